# Optimizing a Trainium2 kernel written in Bass

```python
import jax, jax.numpy as jnp
from jax import lax
import numpy as np

D_MODEL = 2048
BATCH = 1
SEQ = 8192
DEPTH = 2

POOL_WIDTH = D_MODEL // 2
POOL_WINDOWS = (2, 4, 8, 16)
N_POOL_GROUPS = len(POOL_WINDOWS)
POOL_GROUP = POOL_WIDTH // N_POOL_GROUPS
ATTN_WIDTH = D_MODEL - POOL_WIDTH
HEAD_DIM = 128
N_HEADS = ATTN_WIDTH // HEAD_DIM
DILATED_PATTERNS = ((128, 1), (512, 4), (2048, 16))
IN_WIDTH = POOL_WIDTH + 3 * ATTN_WIDTH
D_FF = 5632
ROPE_THETA = 10000.0
EPS = 1e-6
Q_BLOCK = 128
NEG = -1e30

kernel_name = "hybrid_pool_dilated_attn_macaron"


def rmsnorm(x, g):
    xf = x.astype(jnp.float32)
    y = xf * lax.rsqrt(jnp.mean(xf * xf, axis=-1, keepdims=True) + EPS)
    return (y * g.astype(jnp.float32)).astype(x.dtype)


def swiglu(h, w_gate, w_up, w_down):
    return (jax.nn.silu(h @ w_gate) * (h @ w_up)) @ w_down


def rope(x, cos, sin):
    xf = x.astype(jnp.float32)
    x1, x2 = jnp.split(xf, 2, axis=-1)
    c = cos[None, :, None, :]
    s = sin[None, :, None, :]
    return jnp.concatenate([x1 * c - x2 * s, x2 * c + x1 * s], axis=-1).astype(x.dtype)


def pool_mixer(u, pool_w, pool_scale):
    B, S, C = u.shape
    uf = u.astype(jnp.float32)
    cs = jnp.concatenate([jnp.zeros((B, 1, C), jnp.float32), jnp.cumsum(uf, axis=1)], axis=1)
    pos = jnp.arange(S)
    groups = []
    for g, w in enumerate(POOL_WINDOWS):
        lo = jnp.clip(pos - w // 2, 0, S)
        hi = jnp.clip(pos + w // 2, 0, S)
        cnt = (hi - lo).astype(jnp.float32)[None, :, None]
        csg = cs[:, :, g * POOL_GROUP:(g + 1) * POOL_GROUP]
        mean = (jnp.take(csg, hi, axis=1) - jnp.take(csg, lo, axis=1)) / cnt
        groups.append(mean - uf[:, :, g * POOL_GROUP:(g + 1) * POOL_GROUP])
    pooled = jnp.stack(groups, axis=2)
    mixed = jnp.einsum('bsgc,gcd->bsgd', pooled, pool_w.astype(jnp.float32))
    out = mixed.reshape(B, S, C) * pool_scale.astype(jnp.float32)
    return out.astype(u.dtype)


def dilated_attention(q, k, v):
    B, S, H, Dh = q.shape
    n_blocks = S // Q_BLOCK
    scale = Dh ** -0.5
    offsets = [d * jnp.arange(-(w // 2) // d, (w // 2) // d + 1) for (w, d) in DILATED_PATTERNS]

    def block(start):
        qpos = start + jnp.arange(Q_BLOCK)
        qb = lax.dynamic_slice_in_dim(q, start, Q_BLOCK, axis=1)
        lses, outs = [], []
        for off in offsets:
            kpos = qpos[:, None] + off[None, :]
            valid = (kpos >= 0) & (kpos < S)
            idx = jnp.clip(kpos, 0, S - 1)
            kg = k[:, idx]
            vg = v[:, idx]
            s = jnp.einsum('bqhd,bqkhd->bhqk', qb, kg,
                           preferred_element_type=jnp.float32) * scale
            s = jnp.where(valid[None, None], s, NEG)
            lse = jax.nn.logsumexp(s, axis=-1)
            p = jnp.exp(s - lse[..., None])
            o = jnp.einsum('bhqk,bqkhd->bqhd', p, vg.astype(jnp.float32))
            lses.append(lse)
            outs.append(o)
        wts = jax.nn.softmax(jnp.stack(lses, axis=0), axis=0)
        wts = jnp.transpose(wts, (0, 1, 3, 2))[..., None]
        return jnp.sum(wts * jnp.stack(outs, axis=0), axis=0).astype(q.dtype)

    out = lax.map(block, jnp.arange(n_blocks) * Q_BLOCK)
    return jnp.transpose(out, (1, 0, 2, 3, 4)).reshape(B, S, H, Dh)


def hybrid_mixer(h, w_in, q_gain, k_gain, pool_w, pool_scale, w_out, cos, sin):
    B, S, _ = h.shape
    proj = h @ w_in
    u = proj[..., :POOL_WIDTH]
    q, k, v = jnp.split(proj[..., POOL_WIDTH:], 3, axis=-1)
    q = q.reshape(B, S, N_HEADS, HEAD_DIM)
    k = k.reshape(B, S, N_HEADS, HEAD_DIM)
    v = v.reshape(B, S, N_HEADS, HEAD_DIM)
    q = rope(rmsnorm(q, q_gain), cos, sin)
    k = rope(rmsnorm(k, k_gain), cos, sin)
    a_out = dilated_attention(q, k, v).reshape(B, S, ATTN_WIDTH)
    p_out = pool_mixer(u, pool_w, pool_scale)
    return jnp.concatenate([p_out, a_out.astype(h.dtype)], axis=-1) @ w_out


def setup_inputs(seed: int = 0) -> dict:
    key = jax.random.key(seed)
    ks = jax.random.split(key, 20)
    f32 = jnp.float32

    def w(k, shape, fan_in):
        return jax.random.normal(k, shape, f32) * fan_in ** -0.5

    def gain(k, shape):
        return 1.0 + 0.05 * jax.random.normal(k, shape, f32)

    return {
        "x": jax.random.normal(ks[0], (BATCH, SEQ, D_MODEL), f32),
        "norm_ffn1": gain(ks[1], (DEPTH, D_MODEL)),
        "ffn1_w_gate": w(ks[2], (DEPTH, D_MODEL, D_FF), D_MODEL),
        "ffn1_w_up": w(ks[3], (DEPTH, D_MODEL, D_FF), D_MODEL),
        "ffn1_w_down": w(ks[4], (DEPTH, D_FF, D_MODEL), D_FF),
        "norm_mix": gain(ks[5], (DEPTH, D_MODEL)),
        "w_in": w(ks[6], (DEPTH, D_MODEL, IN_WIDTH), D_MODEL),
        "q_norm": gain(ks[7], (DEPTH, HEAD_DIM)),
        "k_norm": gain(ks[8], (DEPTH, HEAD_DIM)),
        "pool_w": w(ks[9], (DEPTH, N_POOL_GROUPS, POOL_GROUP, POOL_GROUP), POOL_GROUP),
        "pool_scale": 1.0 + 0.1 * jax.random.normal(ks[10], (DEPTH, POOL_WIDTH), f32),
        "w_out": w(ks[11], (DEPTH, D_MODEL, D_MODEL), D_MODEL),
        "norm_ffn2": gain(ks[12], (DEPTH, D_MODEL)),
        "ffn2_w_gate": w(ks[13], (DEPTH, D_MODEL, D_FF), D_MODEL),
        "ffn2_w_up": w(ks[14], (DEPTH, D_MODEL, D_FF), D_MODEL),
        "ffn2_w_down": w(ks[15], (DEPTH, D_FF, D_MODEL), D_FF),
    }


def reference(x, norm_ffn1, ffn1_w_gate, ffn1_w_up, ffn1_w_down, norm_mix, w_in,
              q_norm, k_norm, pool_w, pool_scale, w_out, norm_ffn2, ffn2_w_gate,
              ffn2_w_up, ffn2_w_down):
    S = x.shape[1]
    inv_freq = ROPE_THETA ** (-jnp.arange(0, HEAD_DIM, 2, dtype=jnp.float32) / HEAD_DIM)
    ang = jnp.arange(S, dtype=jnp.float32)[:, None] * inv_freq[None, :]
    cos, sin = jnp.cos(ang), jnp.sin(ang)
    for l in range(DEPTH):
        h = rmsnorm(x, norm_ffn1[l])
        x = x + (0.5 * swiglu(h, ffn1_w_gate[l], ffn1_w_up[l], ffn1_w_down[l])).astype(x.dtype)
        h = rmsnorm(x, norm_mix[l])
        x = x + hybrid_mixer(h, w_in[l], q_norm[l], k_norm[l], pool_w[l], pool_scale[l],
                             w_out[l], cos, sin).astype(x.dtype)
        h = rmsnorm(x, norm_ffn2[l])
        x = x + (0.5 * swiglu(h, ffn2_w_gate[l], ffn2_w_up[l], ffn2_w_down[l])).astype(x.dtype)
    return x
```

```python
import numpy as np
from contextlib import ExitStack

import concourse.bass as bass
import concourse.mybir as mybir
from concourse.bass_utils import run_bass_kernel_spmd

F32 = mybir.dt.float32
BF16 = mybir.dt.bfloat16
I32 = mybir.dt.int32
AF = mybir.ActivationFunctionType
ALU = mybir.AluOpType

NCORES = 8
D = 2048
KC = D // 128
S = 8192
T = S // NCORES
TW = 512
NTT = T // TW
DFF = 5632
FC = DFF // 128
G = 4
NG = FC // G
EPS = 1e-6
NGU = 4
NWD = 2 * G


class Eng:
    def __init__(self, ctx, name, h):
        self.ctx = ctx
        self.name = name
        self.h = h
        self.sem = ctx.sem(name + "_prog")
        self.n = 0
        self.waited = {}

    def wait(self, *toks):
        for tok in toks:
            if tok is None:
                continue
            sem, v = tok
            key = id(sem)
            if self.waited.get(key, 0) >= v:
                continue
            self.h.wait_ge(sem, v)
            self.waited[key] = v

    def mark(self, instr):
        self.n += 1
        instr.then_inc(self.sem, 1)
        return (self.sem, self.n)


class DmaSem:
    def __init__(self, ctx, name):
        self.sem = ctx.sem(name)
        self.n = 0
        ctx.dma_sems.append(self)

    def add(self, instr):
        self.n += 16
        instr.then_inc(self.sem, 16)
        return (self.sem, self.n)

    def tok(self):
        return (self.sem, self.n)


class Ctx:
    def __init__(self, nc):
        self.nc = nc
        self.es = ExitStack()
        self.scopes = []
        self.dma_sems = []
        self._nsem = 0
        self.pe = Eng(self, "pe", nc.tensor)
        self.act = Eng(self, "act", nc.scalar)
        self.dve = Eng(self, "dve", nc.vector)
        self.pool = Eng(self, "pool", nc.gpsimd)
        self.sp = Eng(self, "sp", nc.sync)

    def sem(self, name):
        self._nsem += 1
        return self.es.enter_context(self.nc.semaphore(f"{name}_m{self._nsem}"))

    def sbuf(self, name, shape, dt):
        es = self.scopes[-1] if self.scopes else self.es
        self._nsem += 1
        return es.enter_context(self.nc.sbuf_tensor(f"{name}_s{self._nsem}", shape, dt))

    def push(self):
        self.scopes.append(ExitStack())

    def pop(self):
        self.barrier()
        self.scopes.pop().close()

    def engines(self):
        return [self.pe, self.act, self.dve, self.pool, self.sp]

    def barrier(self):
        toks = [(e.sem, e.n) for e in self.engines() if e.n > 0]
        toks += [d.tok() for d in self.dma_sems if d.n > 0]
        for e in self.engines():
            e.wait(*toks)

    def psum(self, name, shape, dt=F32):
        return self.es.enter_context(self.nc.psum_tensor(name, shape, dt))

    def close(self):
        while self.scopes:
            self.scopes.pop().close()
        self.es.close()


class Bank:
    def __init__(self, ctx, name):
        self.t = ctx.psum(name, [128, TW], F32)
        self.free = None


def tsl(tt):
    return slice(tt * TW, (tt + 1) * TW)


def rmsnorm_to_h(c, st, g_col):
    nc = c.nc
    x, h = st["x"], st["h"]
    sq, ones, rstd = st["sq"], st["ones"], st["rstd"]
    banks = st["banks"]
    x_ready = st["x_tok"]
    h_free = st.get("h_free")
    nsq = len(sq)
    sq_free = st.setdefault("sq_free", [None] * nsq)
    it = st.setdefault("sq_it", 0)
    for tt in range(NTT):
        bank = banks[tt]
        c.pe.wait(bank.free)
        for kc in range(KC):
            b = it % nsq
            it += 1
            c.act.wait(x_ready, sq_free[b])
            tok = c.act.mark(nc.scalar.activation(out=sq[b][:], in_=x[:, kc, tsl(tt)], func=AF.Square))
            c.pe.wait(tok)
            ins = nc.tensor.matmul(bank.t[:], ones[:], sq[b][:], start=(kc == 0), stop=(kc == KC - 1))
            sq_free[b] = c.pe.mark(ins)
        tok_ss = sq_free[(it - 1) % nsq]
        c.act.wait(tok_ss, st.get("rstd_free"))
        tok = c.act.mark(nc.scalar.activation(out=rstd[:, tsl(tt)], in_=bank.t[:], func=AF.Sqrt,
                                              bias=st["eps_col"][:], scale=1.0 / D))
        bank.free = tok
        c.dve.wait(tok)
        tok = c.dve.mark(nc.vector.reciprocal(out=rstd[:, tsl(tt)], in_=rstd[:, tsl(tt)]))
        c.dve.wait(tok, h_free, x_ready)
        for kc in range(KC):
            ins = nc.vector.scalar_tensor_tensor(out=h[:, kc, tsl(tt)], in0=x[:, kc, tsl(tt)],
                                                 scalar=g_col[:, kc:kc + 1], in1=rstd[:, tsl(tt)],
                                                 op0=ALU.mult, op1=ALU.mult)
        st["h_tok_tt"][tt] = c.dve.mark(ins)
    st["sq_it"] = it
    st["rstd_free"] = st["h_tok_tt"][NTT - 1]


def ffn(c, st, wg, wu, wd, g_col):
    nc = c.nc
    x, h = st["x"], st["h"]
    rmsnorm_to_h(c, st, g_col)
    wgu, wds, actb, sg = st["wgu"], st["wds"], st["actb"], st["sg"]
    gu_banks, d_banks = st["gu_banks"], st["d_banks"]
    gu_free = st["gu_free"]
    wd_free = st["wd_free"]
    gu_sem, wd_sem = st["gu_sem"], st["wd_sem"]
    act_free = st.setdefault("act_free", [None, None])
    sg_free = st.setdefault("sg_free", [None, None])
    act_tok = [None, None]
    unit = st.setdefault("gu_unit", 0)
    dcnt = st.setdefault("d_cnt", 0)
    x_tok_new = None

    def load_gu(f):
        s = f % NGU
        c.pool.wait(gu_free[s])
        gu_sem[s].add(nc.gpsimd.dma_start(out=wgu[s][:, 0, :], in_=wg[f]))
        gu_sem[s].add(nc.gpsimd.dma_start(out=wgu[s][:, 1, :], in_=wu[f]))
        return gu_sem[s].tok()

    def load_wd(g):
        toks = []
        for fi in range(G):
            f = g * G + fi
            s = f % NWD
            c.pool.wait(wd_free[s])
            toks.append(wd_sem[s].add(nc.gpsimd.dma_start(out=wds[s][:], in_=wd[f * 128:(f + 1) * 128, :])))
        return toks

    def gu(f, wtok):
        nonlocal unit
        s = f % NGU
        g, fi = divmod(f, G)
        ab = g % 2
        c.pe.wait(wtok)
        toks = []
        for which in range(2):
            pair = gu_banks[unit % 3]
            unit += 1
            c.pe.wait(pair[0].free, pair[1].free, st["h_tok_tt"][0], st["h_tok_tt"][1])
            for kc in range(KC):
                for tt in range(NTT):
                    ins = nc.tensor.matmul(pair[tt].t[:], wgu[s][:, which, kc * 128:(kc + 1) * 128],
                                           h[:, kc, tsl(tt)], start=(kc == 0), stop=(kc == KC - 1))
            toks.append((c.pe.mark(ins), pair))
        gu_free[s] = toks[1][0]
        st["h_free"] = toks[1][0]
        (tokG, pairG), (tokU, pairU) = toks
        sb = f % 2
        c.act.wait(tokG, sg_free[sb])
        for tt in range(NTT):
            ins = nc.scalar.activation(out=sg[sb][:, tsl(tt)], in_=pairG[tt].t[:], func=AF.Silu)
        tokS = c.act.mark(ins)
        pairG[0].free = tokS
        pairG[1].free = tokS
        c.dve.wait(tokU, tokS, act_free[ab] if fi == 0 else None)
        for tt in range(NTT):
            ins = nc.vector.tensor_tensor(out=actb[ab][:, fi, tsl(tt)], in0=pairU[tt].t[:],
                                          in1=sg[sb][:, tsl(tt)], op=ALU.mult)
        tokA = c.dve.mark(ins)
        pairU[0].free = tokA
        pairU[1].free = tokA
        sg_free[sb] = tokA
        act_tok[ab] = tokA

    def down(g, wtoks):
        nonlocal dcnt, x_tok_new
        ab = g % 2
        c.pe.wait(act_tok[ab], *wtoks)
        for j in range(KC):
            for tt in range(NTT):
                bank = d_banks[dcnt % len(d_banks)]
                dcnt += 1
                c.pe.wait(bank.free)
                for fi in range(G):
                    f = g * G + fi
                    ins = nc.tensor.matmul(bank.t[:], wds[f % NWD][:, j * 128:(j + 1) * 128],
                                           actb[ab][:, fi, tsl(tt)], start=(fi == 0), stop=(fi == G - 1))
                tokD = c.pe.mark(ins)
                c.dve.wait(tokD)
                ins = nc.vector.scalar_tensor_tensor(out=x[:, j, tsl(tt)], in0=bank.t[:], scalar=0.5,
                                                     in1=x[:, j, tsl(tt)], op0=ALU.mult, op1=ALU.add)
                bank.free = c.dve.mark(ins)
        x_tok_new = bank.free
        act_free[ab] = tokD
        for fi in range(G):
            wd_free[(g * G + fi) % NWD] = tokD

    pend = None
    for f in range(FC):
        g, fi = divmod(f, G)
        wtok = load_gu(f)
        if fi == 0:
            wd_toks = load_wd(g)
        gu(f, wtok)
        if fi == 0 and pend is not None:
            down(*pend)
            pend = None
        if fi == G - 1:
            pend = (g, wd_toks)
    down(*pend)
    st["gu_unit"] = unit
    st["d_cnt"] = dcnt
    st["x_tok"] = x_tok_new


NL = 2
HD = 128
NH = 8
PW_L = 64
P_N1, P_NM, P_N2, P_QG, P_QGS, P_KG, P_KGS, P_PS = 0, 16, 32, 48, 49, 50, 51, 52
NVT = 53
UW = T + 16
ATT_SCALE = HD ** -0.5
ATT_PATTERNS = ((0, 1), (1, 4), (2, 16))
NEGB = -30000.0


def alloc_common(c, consts_ap, params_ap):
    nc = c.nc
    st = {}
    st["x"] = c.sbuf("x_sb", [128, KC, T], F32)
    st["h"] = c.sbuf("h_sb", [128, KC, T], BF16)
    st["ones"] = c.sbuf("ones", [128, 128], F32)
    st["ones_bf"] = c.sbuf("ones_bf", [128, 128], BF16)
    st["eps_col"] = c.sbuf("eps_col", [128, 1], F32)
    st["sign_col"] = c.sbuf("sign_col", [128, 1], F32)
    st["rstd"] = c.sbuf("rstd", [128, T], F32)
    st["sq"] = [c.sbuf(f"sq{i}", [128, TW], F32) for i in range(2)]
    st["params"] = c.sbuf("params_sb", [128, NL * PW_L], F32)
    st["identperm"] = c.sbuf("identperm", [128, 256], F32)
    st["mask2"] = c.sbuf("mask2", [128, 256], BF16)
    st["h_tok_tt"] = [None] * NTT
    banks = [Bank(c, f"bank{i}") for i in range(8)]
    st["all_banks"] = banks
    st["gu_banks"] = [(banks[0], banks[1]), (banks[2], banks[3]), (banks[4], banks[5])]
    st["d_banks"] = [banks[6], banks[7]]
    st["banks"] = [banks[6], banks[7]]
    c.dve.mark(nc.vector.memset(st["ones"][:], 1.0))
    c.dve.mark(nc.vector.memset(st["ones_bf"][:], 1.0))
    c.dve.mark(nc.vector.memset(st["eps_col"][:], EPS))
    c.dve.mark(nc.vector.memset(st["sign_col"][0:64, :], -1.0))
    c.dve.mark(nc.vector.memset(st["sign_col"][64:128, :], 1.0))
    ld = DmaSem(c, "const_ld")
    ld.add(nc.sync.dma_start(out=st["params"][:], in_=params_ap))
    ld.add(nc.sync.dma_start(out=st["identperm"][:], in_=consts_ap[:, 0:256]))
    ld2 = DmaSem(c, "const_ld_sw")
    ld2.add(nc.gpsimd.dma_start(out=st["mask2"][:], in_=consts_ap[:, 256:512]))
    st["gu_sem"] = [DmaSem(c, f"gu_sem{i}") for i in range(NGU)]
    st["wd_sem"] = [DmaSem(c, f"wd_sem{i}") for i in range(NWD)]
    st["ring_sem"] = [DmaSem(c, f"ring_sem{i}") for i in range(4)]
    st["gu_free"] = [None] * NGU
    st["wd_free"] = [None] * NWD
    st["ring_free"] = [None] * 4
    st["misc_sem"] = DmaSem(c, "misc_ld")
    c.barrier()
    return st


def alloc_ffn(c, st):
    st["wgu"] = [c.sbuf(f"wgu{i}", [128, 2, D], BF16) for i in range(NGU)]
    st["wds"] = [c.sbuf(f"wd{i}", [128, D], BF16) for i in range(NWD)]
    st["actb"] = [c.sbuf(f"act{i}", [128, G, T], BF16) for i in range(2)]
    st["sg"] = [c.sbuf(f"sg{i}", [128, T], F32) for i in range(2)]


def run_ffn(c, st, wg, wu, wd, g_col):
    c.push()
    alloc_ffn(c, st)
    ffn(c, st, wg, wu, wd, g_col)
    c.pop()


def pcol(st, l, off, n=1):
    return st["params"][:, l * PW_L + off: l * PW_L + off + n]


def mixer_A(c, st, mx, l, w_in_t, cos_ap, sin_ap, kblk, vblk):
    nc = c.nc
    x, h = st["x"], st["h"]
    rmsnorm_to_h(c, st, pcol(st, l, P_NM, KC))
    qT, uT, ring = mx["qT"], mx["uT"], mx["ring"]
    ring_sem, ring_free = st["ring_sem"], st["ring_free"]
    gu_banks, d_banks = st["gu_banks"], st["d_banks"]
    ident = st["identperm"][:, 0:128]
    perm = st["identperm"][:, 128:256]
    ones = st["ones"]
    sq = st["sq"]
    sq_free = st["sq_free"]
    c.push()
    cosT = c.sbuf("cosT", [128, T], F32)
    sinT = c.sbuf("sinT", [128, T], F32)
    q_sb = [c.sbuf(f"q_sb{i}", [128, TW], F32) for i in range(2)]
    rtq = [c.sbuf(f"rtq{i}", [128, TW], F32) for i in range(2)]
    t1 = [c.sbuf(f"t1_{i}", [128, TW], F32) for i in range(2)]
    t2 = c.sbuf("t2", [128, TW], F32)
    kst = [c.sbuf(f"kst{i}", [128, T], BF16) for i in range(2)]
    vT_sb = c.sbuf("vT_sb", [128, T], F32)
    v_st = [c.sbuf(f"v_st{i}", [128, 8, 128], BF16) for i in range(2)]
    gsgn = c.sbuf("gsgn", [128, 2], F32)
    misc = st["misc_sem"]
    misc.add(nc.sync.dma_start(out=cosT[:], in_=cos_ap))
    tab_tok = misc.add(nc.sync.dma_start(out=sinT[:], in_=sin_ap))
    c.dve.wait(tab_tok)
    nc.vector.tensor_tensor(out=gsgn[:, 0:1], in0=pcol(st, l, P_QGS), in1=st["sign_col"][:], op=ALU.mult)
    gs_tok = c.dve.mark(nc.vector.tensor_tensor(out=gsgn[:, 1:2], in0=pcol(st, l, P_KGS), in1=st["sign_col"][:], op=ALU.mult))
    c.dve.wait(gs_tok)
    unit = st["gu_unit"]
    dcnt = st["d_cnt"]
    it = st["sq_it"]
    nsq = len(sq)
    qi = 0
    kst_dma = [DmaSem(c, f"kst_dma{l}_{i}") for i in range(2)]
    vst_dma = [DmaSem(c, f"vst_dma{l}_{i}") for i in range(2)]
    qsb_free = [None, None]
    rtq_free = [None, None]
    t1_free = [None, None]
    t2_free = None
    vT_free = None
    for oc in range(32):
        s = oc % 4
        c.pool.wait(ring_free[s])
        wtok = ring_sem[s].add(nc.gpsimd.dma_start(out=ring[s][:], in_=w_in_t[oc]))
        pair = gu_banks[unit % 3]
        unit += 1
        c.pe.wait(wtok, pair[0].free, pair[1].free, st["h_tok_tt"][0], st["h_tok_tt"][1])
        for kc in range(KC):
            for tt in range(NTT):
                ins = nc.tensor.matmul(pair[tt].t[:], ring[s][:, kc * 128:(kc + 1) * 128], h[:, kc, tsl(tt)],
                                       start=(kc == 0), stop=(kc == KC - 1))
        tokP = c.pe.mark(ins)
        ring_free[s] = tokP
        st["h_free"] = tokP
        kind, hd = divmod(oc, 8)
        if kind == 0:
            c.act.wait(tokP)
            for tt in range(NTT):
                ins = nc.scalar.activation(out=uT[:, hd, 8 + tt * TW: 8 + (tt + 1) * TW], in_=pair[tt].t[:], func=AF.Copy)
            tok = c.act.mark(ins)
            pair[0].free = tok
            pair[1].free = tok
            mx["u_tok"] = tok
        elif kind in (1, 2):
            gcol = pcol(st, l, P_QG if kind == 1 else P_KG)
            gscol = gsgn[:, kind - 1:kind]
            kb = hd % 2
            if kind == 2:
                c.dve.wait(kst_dma[kb].tok())
            for tt in range(NTT):
                b = qi % 2
                qi += 1
                b2 = it % nsq
                it += 1
                c.act.wait(tokP, qsb_free[b], sq_free[b2])
                nc.scalar.activation(out=q_sb[b][:], in_=pair[tt].t[:], func=AF.Copy)
                tokA = c.act.mark(nc.scalar.activation(out=sq[b2][:], in_=pair[tt].t[:], func=AF.Square))
                pair[tt].free = tokA
                bS = d_banks[dcnt % 2]
                bW = d_banks[(dcnt + 1) % 2]
                dcnt += 2
                c.pe.wait(tokA, bS.free, bW.free)
                nc.tensor.matmul(bS.t[:], ones[:], sq[b2][:], start=True, stop=True)
                tokM = c.pe.mark(nc.tensor.matmul(bW.t[:], perm, q_sb[b][:], start=True, stop=True))
                sq_free[b2] = tokM
                c.act.wait(tokM, rtq_free[b])
                tokR = c.act.mark(nc.scalar.activation(out=rtq[b][:], in_=bS.t[:], func=AF.Sqrt,
                                                       bias=st["eps_col"][:], scale=1.0 / HD))
                bS.free = tokR
                c.dve.wait(tokR, tokA, tokM, t1_free[b], t2_free)
                tk = c.dve.mark(nc.vector.reciprocal(out=rtq[b][:], in_=rtq[b][:]))
                nc.vector.scalar_tensor_tensor(out=t1[b][:], in0=q_sb[b][:], scalar=gcol, in1=cosT[:, tsl(tt)],
                                               op0=ALU.mult, op1=ALU.mult)
                tk2 = c.dve.mark(nc.vector.scalar_tensor_tensor(out=t2[:], in0=bW.t[:], scalar=gscol, in1=sinT[:, tsl(tt)],
                                                                op0=ALU.mult, op1=ALU.mult))
                bW.free = tk2
                qsb_free[b] = tk2
                c.dve.wait(tk2)
                tk3 = c.dve.mark(nc.vector.tensor_tensor(out=t1[b][:], in0=t1[b][:], in1=t2[:], op=ALU.add))
                t2_free = tk3
                c.dve.wait(tk3)
                dst = qT[:, hd, tsl(tt)] if kind == 1 else kst[kb][:, tsl(tt)]
                tk4 = c.dve.mark(nc.vector.tensor_tensor(out=dst, in0=t1[b][:], in1=rtq[b][:], op=ALU.mult))
                t1_free[b] = tk4
                rtq_free[b] = tk4
            if kind == 1:
                mx["q_tok"] = tk4
            else:
                c.sp.wait(tk4)
                kst_dma[kb].add(nc.sync.dma_start(out=kblk[hd * 128:(hd + 1) * 128, :], in_=kst[kb][:]))
        else:
            vb = hd % 2
            c.act.wait(tokP, vT_free)
            for tt in range(NTT):
                ins = nc.scalar.activation(out=vT_sb[:, tsl(tt)], in_=pair[tt].t[:], func=AF.Copy)
            tokV = c.act.mark(ins)
            pair[0].free = tokV
            pair[1].free = tokV
            for half in range(2):
                bT = d_banks[dcnt % 2]
                dcnt += 1
                c.pe.wait(tokV, bT.free)
                for ti in range(4):
                    tix = half * 4 + ti
                    ins = nc.tensor.transpose(bT.t[:, ti * 128:(ti + 1) * 128], vT_sb[:, tix * 128:(tix + 1) * 128], ident)
                tokT = c.pe.mark(ins)
                if half == 1:
                    vT_free = tokT
                if half == 0:
                    c.dve.wait(vst_dma[vb].tok())
                c.dve.wait(tokT)
                tokC = c.dve.mark(nc.vector.tensor_copy(out=v_st[vb][:, half * 4:(half + 1) * 4, :],
                                                        in_=bT.t[:].rearrange("p (a e) -> p a e", e=128)))
                bT.free = tokC
            c.sp.wait(tokC)
            vst_dma[vb].add(nc.sync.dma_start(
                out=vblk.rearrange("(ti p) c -> p ti c", p=128)[:, :, hd * 128:(hd + 1) * 128], in_=v_st[vb][:]))
    st["gu_unit"] = unit
    st["d_cnt"] = dcnt
    st["sq_it"] = it
    c.pop()


def pool_mixer(c, st, mx, l, pool_w_ap, invcnt_ap):
    nc = c.nc
    uT = mx["uT"]
    cat = st["h"]
    d_banks = st["d_banks"]
    c.push()
    Wb = [c.sbuf(f"poolW{i}", [128, 2, UW], F32) for i in range(2)]
    pooled = c.sbuf("pooled", [128, 2, T], BF16)
    icnt = c.sbuf("icnt", [128, T], F32)
    pw = [c.sbuf(f"pw{i}", [128, 2, 256], BF16) for i in range(2)]
    icnt_sem = DmaSem(c, f"icnt_sem{l}")
    pw_sem = [DmaSem(c, f"pw_sem{l}_{i}") for i in range(2)]
    dcnt = st["d_cnt"]
    c.dve.wait(mx["u_tok"], mx.get("uhalo_tok"))
    pooled_free = None
    icnt_free = None
    pw_free = [None, None]
    P = c.dve
    for g in range(4):
        ch = slice(2 * g, 2 * g + 2)
        c.sp.wait(icnt_free)
        itok = icnt_sem.add(nc.sync.dma_start(out=icnt[:], in_=invcnt_ap[g]))
        c.pool.wait(pw_free[g % 2])
        ptok = pw_sem[g % 2].add(nc.gpsimd.dma_start(out=pw[g % 2][:], in_=pool_w_ap[g].rearrange("(cc p) d -> p cc d", p=128)))
        P.wait(st.get("pool_prev"))
        tk = P.mark(nc.vector.tensor_tensor(out=Wb[0][:, :, 1:UW], in0=uT[:, ch, 0:UW - 1], in1=uT[:, ch, 1:UW], op=ALU.add))
        cur = 0
        lo, hi, sh = 1, UW, 1
        for lev in range(g):
            P.wait(tk)
            nlo, nhi = lo + sh, hi - sh
            tk = P.mark(nc.vector.tensor_tensor(out=Wb[1 - cur][:, :, nlo:nhi], in0=Wb[cur][:, :, nlo - sh:nhi - sh],
                                                in1=Wb[cur][:, :, nlo + sh:nhi + sh], op=ALU.add))
            cur = 1 - cur
            lo, hi, sh = nlo, nhi, sh * 2
        P.wait(tk, itok)
        for cc in range(2):
            tk = P.mark(nc.vector.tensor_tensor(out=Wb[cur][:, cc, 8:8 + T], in0=Wb[cur][:, cc, 8:8 + T], in1=icnt[:], op=ALU.mult))
        icnt_free = tk
        P.wait(tk, pooled_free)
        tkp = P.mark(nc.vector.tensor_tensor(out=pooled[:], in0=Wb[cur][:, :, 8:8 + T], in1=uT[:, ch, 8:8 + T], op=ALU.subtract))
        st["pool_prev"] = tkp
        c.pe.wait(tkp, ptok)
        for dc in range(2):
            for tt in range(NTT):
                bank = d_banks[dcnt % 2]
                dcnt += 1
                c.pe.wait(bank.free)
                for cc in range(2):
                    ins = nc.tensor.matmul(bank.t[:], pw[g % 2][:, cc, dc * 128:(dc + 1) * 128], pooled[:, cc, tsl(tt)],
                                           start=(cc == 0), stop=(cc == 1))
                tokM = c.pe.mark(ins)
                c.act.wait(tokM)
                tokE = c.act.mark(nc.scalar.activation(out=cat[:, 2 * g + dc, tsl(tt)], in_=bank.t[:], func=AF.Identity,
                                                       scale=pcol(st, l, P_PS + 2 * g + dc)))
                bank.free = tokE
        pooled_free = tokM
        pw_free[g % 2] = tokM
    st["d_cnt"] = dcnt
    c.pop()


def attention(c, st, mx, l, kwin, vwin, kbias_ap):
    nc = c.nc
    qT = mx["qT"]
    cat = st["h"]
    banks = st["all_banks"]
    mask2 = st["mask2"]
    ones_bf = st["ones_bf"]
    c.push()
    kT_h = [c.sbuf(f"kT_h{i}", [128, 3 * T], BF16) for i in range(2)]
    vt = [c.sbuf(f"vt{i}", [128, NVT, 128], BF16) for i in range(2)]
    E = [c.sbuf(f"E{i}", [128, 256], BF16) for i in range(4)]
    accn = c.sbuf("accn", [128, T], F32)
    accd = c.sbuf("accd", [128, T], F32)
    kbias = c.sbuf("kbias", [128, NVT], F32)
    kv_sem = [DmaSem(c, f"kv_sem{l}_{i}") for i in range(2)]
    kb_tok = st["misc_sem"].add(nc.sync.dma_start(out=kbias[:], in_=kbias_ap))
    c.act.wait(kb_tok)
    num_ps = [banks[0], banks[1]]
    den_ps = [banks[2], banks[3]]
    st_slots = [(banks[4], 0), (banks[5], 0), (banks[6], 0), (banks[7], 0)]
    st_free = [None] * 4
    E_free = [None] * 4
    kv_free = [None, None]
    acc_free = None
    vrows = vwin.rearrange("j t c -> (j t) c")
    c.pe.wait(mx["q_tok"])
    slot_i = 0
    for hd in range(NH):
        b = hd % 2
        c.sp.wait(kv_free[b])
        kv_sem[b].add(nc.sync.dma_start(out=kT_h[b][:].rearrange("p (j t) -> p j t", j=3),
                                        in_=kwin[:, hd * 128:(hd + 1) * 128, :].rearrange("j p t -> p j t")))
        hs = slice(hd * 128, (hd + 1) * 128)
        kv_sem[b].add(nc.sync.dma_start(out=vt[b][:, 0:9, :],
                                        in_=vrows[960:960 + 9 * 128, hs].rearrange("(m p) e -> p m e", p=128)))
        for r in range(4):
            kv_sem[b].add(nc.sync.dma_start(
                out=vt[b][:, 9 + 3 * r: 12 + 3 * r, :],
                in_=vrows[768:768 + 1536, hs].rearrange("(m p r) e -> p r m e", p=128, r=4)[:, r, :, :]))
        for r in range(16):
            kv_sem[b].add(nc.sync.dma_start(
                out=vt[b][:, 21 + 2 * r, :],
                in_=vrows[0:2048, hs].rearrange("(p r) e -> p r e", r=16)[:, r, :]))
            kv_sem[b].add(nc.sync.dma_start(
                out=vt[b][0:64, 22 + 2 * r, :],
                in_=vrows[2048:3072, hs].rearrange("(p r) e -> p r e", r=16)[:, r, :]))
        kvtok = kv_sem[b].tok()
        c.pe.wait(kvtok)
        kh = kT_h[b]
        first_evac = True
        for (pi, d) in ATT_PATTERNS:
            nsub = T // d
            qv = qT[:, hd, :].rearrange("p (i r) -> p r i", r=d)
            tiles = []
            for r in range(d):
                if d == 16:
                    tiles.append(dict(idx=21 + 2 * r, K=128, w0=r, i0=0, i1=64, mlo=128, r=r))
                    tiles.append(dict(idx=22 + 2 * r, K=64, w0=2048 + r, i0=0, i1=64, mlo=0, r=r))
                else:
                    nqt = nsub // 128
                    base = 1024 - 64 * d
                    for m in range(nqt + 1):
                        i0 = max(0, 128 * (m - 1))
                        i1 = min(nsub, 128 * (m + 1))
                        idx = m if d == 1 else 9 + 3 * r + m
                        tiles.append(dict(idx=idx, K=128, w0=base + d * 128 * m + r, i0=i0, i1=i1,
                                          mlo=(128 if m == 0 else 0), r=r))
            started = {}
            pending = []

            def issue_pv(tl):
                K, N = tl["K"], tl["i1"] - tl["i0"]
                e = tl["e"]
                c.pe.wait(tl["ptok"])
                col0 = tl["r"] * nsub + tl["i0"]
                segs = []
                a = col0
                while a < col0 + N:
                    bnd = min(col0 + N, (a // TW + 1) * TW)
                    segs.append((a, bnd))
                    a = bnd
                for (a, bnd) in segs:
                    bk = a // TW
                    for which, pbanks, lhsT in ((0, num_ps, vt[b][0:K, tl["idx"], :]), (1, den_ps, ones_bf[0:K, :])):
                        key = (which, bk)
                        ins = nc.tensor.matmul(pbanks[bk].t[:, a - bk * TW: bnd - bk * TW], lhsT,
                                               E[e][0:K, a - col0: bnd - col0],
                                               start=(key not in started), stop=False, skip_group_check=True)
                        started[key] = True
                E_free[e] = c.pe.mark(ins)
                return E_free[e]

            last_pv = None
            for tl in tiles:
                K, N = tl["K"], tl["i1"] - tl["i0"]
                sl = slot_i % 4
                slot_i += 1
                sbank, soff = st_slots[sl]
                c.pe.wait(st_free[sl])
                w0 = tl["w0"]
                ins = nc.tensor.matmul(sbank.t[0:K, soff:soff + N], kh[:, w0: w0 + d * (K - 1) + 1: d],
                                       qv[:, tl["r"], tl["i0"]:tl["i1"]], start=True, stop=True, skip_group_check=True)
                tokS = c.pe.mark(ins)
                e = sl
                c.act.wait(tokS, E_free[e])
                tokE = c.act.mark(nc.scalar.activation(out=E[e][0:K, 0:N], in_=sbank.t[0:K, soff:soff + N], func=AF.Exp,
                                                       bias=kbias[0:K, tl["idx"]:tl["idx"] + 1], scale=ATT_SCALE))
                st_free[sl] = tokE
                c.dve.wait(tokE)
                tl["ptok"] = c.dve.mark(nc.vector.tensor_tensor(out=E[e][0:K, 0:N], in0=E[e][0:K, 0:N],
                                                                in1=mask2[0:K, tl["mlo"]:tl["mlo"] + N], op=ALU.mult))
                tl["e"] = e
                pending.append(tl)
                if len(pending) > 2:
                    last_pv = issue_pv(pending.pop(0))
            while pending:
                last_pv = issue_pv(pending.pop(0))
            c.dve.wait(last_pv, acc_free if first_evac else None)
            for (acc, pbanks) in ((accn, num_ps), (accd, den_ps)):
                for bk in range(2):
                    src = pbanks[bk].t[:]
                    if d == 1:
                        ins = nc.vector.tensor_copy(out=acc[:, bk * TW:(bk + 1) * TW], in_=src)
                    else:
                        nr = TW // nsub
                        dst = acc[:].rearrange("p (i r) -> p r i", r=d)[:, bk * nr:(bk + 1) * nr, :]
                        ins = nc.vector.tensor_tensor(out=dst, in0=dst, in1=src.rearrange("p (r i) -> p r i", i=nsub), op=ALU.add)
                    tokV = c.dve.mark(ins)
                    c.dve.wait(tokV)
            first_evac = False
            for bk_ in num_ps + den_ps:
                bk_.free = tokV
            c.pe.wait(tokV)
        kv_free[b] = last_pv
        c.dve.wait(tokV)
        tk = c.dve.mark(nc.vector.reciprocal(out=accd[:], in_=accd[:]))
        c.dve.wait(tk)
        acc_free = c.dve.mark(nc.vector.tensor_tensor(out=cat[:, 8 + hd, :], in0=accn[:], in1=accd[:], op=ALU.mult))
    for bk_ in banks:
        bk_.free = acc_free
    c.pop()


def w_out_proj(c, st, mx, w_out_t):
    nc = c.nc
    x, cat = st["x"], st["h"]
    ring = mx["ring"]
    ring_sem, ring_free = st["ring_sem"], st["ring_free"]
    gu_banks = st["gu_banks"]
    unit = st["gu_unit"]
    for oc in range(KC):
        s = oc % 4
        c.pool.wait(ring_free[s])
        wtok = ring_sem[s].add(nc.gpsimd.dma_start(out=ring[s][:], in_=w_out_t[oc]))
        pair = gu_banks[unit % 3]
        unit += 1
        c.pe.wait(wtok, pair[0].free, pair[1].free)
        for kc in range(KC):
            for tt in range(NTT):
                ins = nc.tensor.matmul(pair[tt].t[:], ring[s][:, kc * 128:(kc + 1) * 128], cat[:, kc, tsl(tt)],
                                       start=(kc == 0), stop=(kc == KC - 1))
        tokP = c.pe.mark(ins)
        ring_free[s] = tokP
        c.dve.wait(tokP)
        for tt in range(NTT):
            ins = nc.vector.tensor_tensor(out=x[:, oc, tsl(tt)], in0=pair[tt].t[:], in1=x[:, oc, tsl(tt)], op=ALU.add)
        tok = c.dve.mark(ins)
        pair[0].free = tok
        pair[1].free = tok
    st["gu_unit"] = unit
    st["x_tok"] = tok
    st["h_free"] = tokP


def dram_in(nc, name, shape, dt=F32):
    return nc.dram_tensor(name, list(shape), dt, kind="ExternalInput").ap()


def dram_out(nc, name, shape, dt=F32):
    return nc.dram_tensor(name, list(shape), dt, kind="ExternalOutput").ap()


def layer_weight_aps(nc, l, need_a, need_b):
    w = {}
    if need_a:
        w["wg1"] = dram_in(nc, f"wg1_{l}", [FC, 128, D])
        w["wu1"] = dram_in(nc, f"wu1_{l}", [FC, 128, D])
        w["wd1"] = dram_in(nc, f"wd1_{l}", [DFF, D])
        w["w_in"] = dram_in(nc, f"w_in_{l}", [32, 128, D])
    if need_b:
        w["pool_w"] = dram_in(nc, f"pool_w_{l}", [4, 256, 256])
        w["w_out"] = dram_in(nc, f"w_out_{l}", [KC, 128, D])
        w["wg2"] = dram_in(nc, f"wg2_{l}", [FC, 128, D])
        w["wu2"] = dram_in(nc, f"wu2_{l}", [FC, 128, D])
        w["wd2"] = dram_in(nc, f"wd2_{l}", [DFF, D])
    return w


def build_unfused(kind, l, parts=("pool", "attn", "wout", "ffn")):
    nc = bass.Bass("TRN2", target_bir_lowering=False)
    consts = dram_in(nc, "consts", [128, 512])
    params = dram_in(nc, "params", [128, NL * PW_L])
    cos_ap = dram_in(nc, "cosT", [128, T])
    sin_ap = dram_in(nc, "sinT", [128, T])
    has_b = kind in ("BA", "B")
    has_a = kind in ("A", "BA")
    la = l if kind == "A" else l + 1
    x_in = dram_in(nc, "x_in", [D, T])
    if has_b:
        q_in = dram_in(nc, "q_in", [NH * 128, T], BF16)
        u_in = dram_in(nc, "u_in", [NH * 128, UW])
        kwin = dram_in(nc, "kwin", [3, NH * 128, T], BF16)
        vwin = dram_in(nc, "vwin", [3, T, NH * 128], BF16)
        invcnt = dram_in(nc, "invcnt", [4, 128, T])
        kbias = dram_in(nc, "kbias", [128, NVT])
        wb = layer_weight_aps(nc, l, False, True)
    if has_a:
        wa = layer_weight_aps(nc, la, True, False)
        q_out = dram_out(nc, "q_out", [NH * 128, T], BF16)
        u_out = dram_out(nc, "u_out", [NH * 128, T])
        kblk = dram_out(nc, "kblk", [NH * 128, T], BF16)
        vblk = dram_out(nc, "vblk", [T, NH * 128], BF16)
    x_out = dram_out(nc, "x_out", [D, T])
    c = Ctx(nc)
    st = alloc_common(c, consts, params)
    st["gu_unit"] = 0
    st["d_cnt"] = 0
    st["sq_it"] = 0
    st["sq_free"] = [None] * 2
    x = st["x"]
    ld = DmaSem(c, "x_ld")
    for kc in range(KC):
        ld.add(nc.sync.dma_start(out=x[:, kc, :], in_=x_in[kc * 128:(kc + 1) * 128, :]))
    st["x_tok"] = ld.tok()
    c.dve.wait(ld.tok())
    if has_b:
        c.push()
        mx = {}
        mx["qT"] = c.sbuf("qT", [128, NH, T], BF16)
        mx["ring"] = [c.sbuf(f"ring{i}", [128, D], BF16) for i in range(4)]
        c.push()
        mx["uT"] = c.sbuf("uT", [128, NH, UW], F32)
        sl = DmaSem(c, "state_ld")
        sl.add(nc.sync.dma_start(out=mx["qT"][:], in_=q_in.rearrange("(h p) t -> p h t", p=128)))
        sl.add(nc.sync.dma_start(out=mx["uT"][:], in_=u_in.rearrange("(h p) t -> p h t", p=128)))
        mx["q_tok"] = sl.tok()
        mx["u_tok"] = sl.tok()
        if "pool" in parts:
            pool_mixer(c, st, mx, l, wb["pool_w"], invcnt)
        c.pop()
        if "attn" in parts:
            attention(c, st, mx, l, kwin, vwin, kbias)
        if "wout" in parts:
            w_out_proj(c, st, mx, wb["w_out"])
        c.pop()
        if "ffn" in parts:
            run_ffn(c, st, wb["wg2"], wb["wu2"], wb["wd2"], pcol(st, l, P_N2, KC))
    if has_a:
        run_ffn(c, st, wa["wg1"], wa["wu1"], wa["wd1"], pcol(st, la, P_N1, KC))
        c.push()
        mx = {}
        mx["qT"] = c.sbuf("qT_a", [128, NH, T], BF16)
        mx["ring"] = [c.sbuf(f"ring_a{i}", [128, D], BF16) for i in range(4)]
        mx["uT"] = c.sbuf("uT_a", [128, NH, UW], F32)
        mixer_A(c, st, mx, la, wa["w_in"], cos_ap, sin_ap, kblk, vblk)
        so = DmaSem(c, "state_st")
        c.sp.wait(mx["q_tok"], mx["u_tok"])
        so.add(nc.sync.dma_start(out=q_out.rearrange("(h p) t -> p h t", p=128), in_=mx["qT"][:]))
        so.add(nc.sync.dma_start(out=u_out.rearrange("(h p) t -> p h t", p=128), in_=mx["uT"][:, :, 8:8 + T]))
        c.sp.wait(so.tok())
        c.pop()
    c.sp.wait(st["x_tok"])
    stsem = DmaSem(c, "x_st")
    for kc in range(KC):
        stsem.add(nc.sync.dma_start(out=x_out[kc * 128:(kc + 1) * 128, :], in_=x[:, kc, :]))
    c.barrier()
    c.close()
    return nc


import ml_dtypes

BF16_NP = ml_dtypes.bfloat16
POOL_WINDOWS = (2, 4, 8, 16)


def tile_w_in(w):
    n = w.shape[1]
    return np.ascontiguousarray(w.reshape(KC, 128, n // 128, 128).transpose(2, 1, 0, 3).reshape(n // 128, 128, D))


def col_layout(v):
    return np.ascontiguousarray(v.reshape(-1, 128).T)


def make_consts():
    p = np.arange(128)
    ident = np.eye(128, dtype=np.float32)
    perm = (p[:, None] == ((p[None, :] + 64) % 128)).astype(np.float32)
    le = (p[:, None] <= p[None, :]).astype(np.float32)
    ge = (p[:, None] >= p[None, :]).astype(np.float32)
    return np.ascontiguousarray(np.concatenate([ident, perm, le, ge], axis=1))


def make_params(inp):
    out = np.zeros((128, NL * PW_L), np.float32)
    for l in range(NL):
        o = l * PW_L
        out[:, o + P_N1:o + P_N1 + KC] = col_layout(inp["norm_ffn1"][l])
        out[:, o + P_NM:o + P_NM + KC] = col_layout(inp["norm_mix"][l])
        out[:, o + P_N2:o + P_N2 + KC] = col_layout(inp["norm_ffn2"][l])
        out[:, o + P_QG] = inp["q_norm"][l]
        out[:, o + P_QGS] = np.roll(inp["q_norm"][l], 64)
        out[:, o + P_KG] = inp["k_norm"][l]
        out[:, o + P_KGS] = np.roll(inp["k_norm"][l], 64)
        out[:, o + P_PS:o + P_PS + 8] = col_layout(inp["pool_scale"][l])
    return out


def make_tables(core):
    t0 = core * T
    inv_freq = (np.float32(10000.0) ** (-np.arange(0, HD, 2, dtype=np.float32) / np.float32(HD))).astype(np.float32)
    pos = np.arange(t0, t0 + T, dtype=np.float32)
    ang = (pos[:, None] * inv_freq[None, :]).astype(np.float32)
    cosT = np.ascontiguousarray(np.concatenate([np.cos(ang), np.cos(ang)], axis=1).T.astype(np.float32))
    sinT = np.ascontiguousarray(np.concatenate([np.sin(ang), np.sin(ang)], axis=1).T.astype(np.float32))
    ipos = np.arange(t0, t0 + T)
    invcnt = np.zeros((4, 128, T), np.float32)
    for g, w in enumerate(POOL_WINDOWS):
        lo = np.clip(ipos - w // 2, 0, S)
        hi = np.clip(ipos + w // 2, 0, S)
        invcnt[g] = (np.float32(1.0) / (hi - lo).astype(np.float32))[None, :]
    kbias = np.zeros((128, NVT), np.float32)
    p = np.arange(128)

    def setb(idx, w0, d):
        tok = t0 - T + w0 + d * p
        kbias[:, idx] = np.where((tok >= 0) & (tok < S), 0.0, NEGB)

    for m in range(9):
        setb(m, 960 + 128 * m, 1)
    for r in range(4):
        for m in range(3):
            setb(9 + 3 * r + m, 768 + 512 * m + r, 4)
    for r in range(16):
        for m in range(2):
            setb(21 + 2 * r + m, 2048 * m + r, 16)
    return dict(cosT=cosT, sinT=sinT, invcnt=invcnt, kbias=kbias)


_PROGS = {}


def get_prog(kind, l):
    key = (kind, l)
    if key not in _PROGS:
        _PROGS[key] = build_unfused(kind, l)
    return _PROGS[key]


def layer_host_weights(inp, l, need_a, need_b):
    w = {}
    if need_a:
        w[f"wg1_{l}"] = tile_w_in(inp["ffn1_w_gate"][l])
        w[f"wu1_{l}"] = tile_w_in(inp["ffn1_w_up"][l])
        w[f"wd1_{l}"] = np.ascontiguousarray(inp["ffn1_w_down"][l])
        w[f"w_in_{l}"] = tile_w_in(inp["w_in"][l])
    if need_b:
        w[f"pool_w_{l}"] = np.ascontiguousarray(inp["pool_w"][l])
        w[f"w_out_{l}"] = tile_w_in(inp["w_out"][l])
        w[f"wg2_{l}"] = tile_w_in(inp["ffn2_w_gate"][l])
        w[f"wu2_{l}"] = tile_w_in(inp["ffn2_w_up"][l])
        w[f"wd2_{l}"] = np.ascontiguousarray(inp["ffn2_w_down"][l])
    return w


def exchange(res):
    kpad = np.zeros((NCORES + 2, NH * 128, T), BF16_NP)
    vpad = np.zeros((NCORES + 2, T, NH * 128), BF16_NP)
    upad = np.zeros((NH * 128, S + 16), np.float32)
    for cidx, r in enumerate(res):
        kpad[cidx + 1] = r["kblk"]
        vpad[cidx + 1] = r["vblk"]
        upad[:, 8 + cidx * T: 8 + (cidx + 1) * T] = r["u_out"]
    outs = []
    for cidx, r in enumerate(res):
        outs.append(dict(x_in=r["x_out"], q_in=r["q_out"],
                         u_in=np.ascontiguousarray(upad[:, cidx * T: cidx * T + UW]),
                         kwin=np.ascontiguousarray(kpad[cidx:cidx + 3]),
                         vwin=np.ascontiguousarray(vpad[cidx:cidx + 3])))
    return outs


def kernel(**inp):
    inp = {k: np.asarray(v) for k, v in inp.items()}
    x = inp["x"][0]
    consts = make_consts()
    params = make_params(inp)
    tabs = [make_tables(cidx) for cidx in range(NCORES)]
    cores = list(range(NCORES))
    base = [dict(consts=consts, params=params, cosT=tabs[i]["cosT"], sinT=tabs[i]["sinT"]) for i in cores]
    btab = [dict(invcnt=tabs[i]["invcnt"], kbias=tabs[i]["kbias"]) for i in cores]
    wa = layer_host_weights(inp, 0, True, False)
    maps = [dict(base[i], x_in=np.ascontiguousarray(x[i * T:(i + 1) * T].T), **wa) for i in cores]
    res = run_bass_kernel_spmd(get_prog("A", 0), maps, core_ids=cores).results
    ex = exchange(res)
    wb = layer_host_weights(inp, 0, False, True)
    wa = layer_host_weights(inp, 1, True, False)
    maps = [dict(base[i], **btab[i], **ex[i], **wb, **wa) for i in cores]
    res = run_bass_kernel_spmd(get_prog("BA", 0), maps, core_ids=cores).results
    ex = exchange(res)
    wb = layer_host_weights(inp, 1, False, True)
    maps = [dict(base[i], **btab[i], **ex[i], **wb) for i in cores]
    res = run_bass_kernel_spmd(get_prog("B", 1), maps, core_ids=cores).results
    y = np.concatenate([np.asarray(r["x_out"]).T for r in res], axis=0)
    return np.ascontiguousarray(y[None].astype(np.float32))
```
